# Optimizing a Trainium2 kernel written in Bass

```python
import jax, jax.numpy as jnp
from jax import lax
import numpy as np

D_MODEL = 1024
BATCH = 16
SEQ = 2048
DEPTH = 2

MIX_WIDTH = D_MODEL
CONV_WIDTH = MIX_WIDTH // 2
SGU_WIDTH = MIX_WIDTH - CONV_WIDTH
HEAD_DIM = 64
CONV_HEADS = CONV_WIDTH // HEAD_DIM
SGU_HEADS = SGU_WIDTH // HEAD_DIM
CONV_K = 3
CHUNK = 128
IN_COLS = 3 * CONV_WIDTH + 2 * SGU_WIDTH
D_FF = ((8 * D_MODEL // 3 + 255) // 256) * 256
N_MOD = 6
EPS = 1e-6

kernel_name = "hybrid_shortconv_chunked_sgu_adaln"


def rmsnorm(x, g):
    xf = x.astype(jnp.float32)
    xf = xf * lax.rsqrt(jnp.mean(xf * xf, axis=-1, keepdims=True) + EPS)
    return xf.astype(x.dtype) * g


def short_conv_mixer(bg, cg, h, conv_w):
    z = cg * h
    zp = jnp.pad(z, ((0, 0), (CONV_K - 1, 0), (0, 0)))
    s = z.shape[1]
    conv = sum(conv_w[k] * zp[:, k:k + s, :] for k in range(CONV_K))
    return bg * conv


def chunked_sgu_mixer(u, v, v_norm, w_s, b_s):
    bsz, s, _ = u.shape
    n_chunks = s // CHUNK
    v = rmsnorm(v, v_norm)
    vc = v.reshape(bsz, n_chunks, CHUNK, SGU_HEADS, HEAD_DIM)
    mask = jnp.tril(jnp.ones((CHUNK, CHUNK), dtype=w_s.dtype))
    ws = w_s * mask[None]
    mixed = jnp.einsum('hij,bcjhd->bcihd', ws, vc)
    mixed = mixed + jnp.transpose(b_s)[None, None, :, :, None]
    return u * mixed.reshape(bsz, s, SGU_WIDTH)


def setup_inputs(seed: int = 0) -> dict:
    key = jax.random.key(seed)
    ks = jax.random.split(key, 20)
    f32 = jnp.float32
    nrm = lambda k, shape, scale: jax.random.normal(k, shape, f32) * scale
    gain = lambda k, shape: 1.0 + 0.02 * jax.random.normal(k, shape, f32)
    return {
        "x": nrm(ks[0], (BATCH, SEQ, D_MODEL), 1.0),
        "c": nrm(ks[1], (BATCH, D_MODEL), 1.0),
        "w_mod": nrm(ks[2], (DEPTH, D_MODEL, N_MOD * D_MODEL), D_MODEL ** -0.5),
        "b_mod": nrm(ks[3], (DEPTH, N_MOD * D_MODEL), 0.01),
        "norm_mix": gain(ks[4], (DEPTH, D_MODEL)),
        "w_in": nrm(ks[5], (DEPTH, D_MODEL, IN_COLS), D_MODEL ** -0.5),
        "conv_w": nrm(ks[6], (DEPTH, CONV_K, CONV_WIDTH), CONV_K ** -0.5),
        "v_norm": gain(ks[7], (DEPTH, SGU_WIDTH)),
        "w_s": nrm(ks[8], (DEPTH, SGU_HEADS, CHUNK, CHUNK), CHUNK ** -0.5),
        "b_s": gain(ks[9], (DEPTH, SGU_HEADS, CHUNK)),
        "out_norm_a": gain(ks[10], (DEPTH, CONV_WIDTH)),
        "out_norm_b": gain(ks[11], (DEPTH, SGU_WIDTH)),
        "w_out": nrm(ks[12], (DEPTH, MIX_WIDTH, D_MODEL), MIX_WIDTH ** -0.5),
        "norm_ffn": gain(ks[13], (DEPTH, D_MODEL)),
        "w_up": nrm(ks[14], (DEPTH, D_MODEL, 2 * D_FF), D_MODEL ** -0.5),
        "w_down": nrm(ks[15], (DEPTH, D_FF, D_MODEL), D_FF ** -0.5),
        "norm_final": gain(ks[16], (D_MODEL,)),
    }


def reference(x, c, w_mod, b_mod, norm_mix, w_in, conv_w, v_norm, w_s, b_s,
              out_norm_a, out_norm_b, w_out, norm_ffn, w_up, w_down, norm_final):
    c_act = jax.nn.silu(c)
    for l in range(DEPTH):
        mod = c_act @ w_mod[l] + b_mod[l]
        sh_m, sc_m, g_m, sh_f, sc_f, g_f = [m[:, None, :] for m in jnp.split(mod, N_MOD, axis=-1)]

        h = rmsnorm(x, norm_mix[l]) * (1.0 + sc_m) + sh_m
        proj = h @ w_in[l]
        bg, cg, hc, u, v = jnp.split(
            proj, np.cumsum([CONV_WIDTH, CONV_WIDTH, CONV_WIDTH, SGU_WIDTH])[:4].tolist(), axis=-1)
        y_a = short_conv_mixer(bg, cg, hc, conv_w[l])
        y_b = chunked_sgu_mixer(jax.nn.gelu(u), jax.nn.gelu(v), v_norm[l], w_s[l], b_s[l])
        y_mix = jnp.concatenate([rmsnorm(y_a, out_norm_a[l]), rmsnorm(y_b, out_norm_b[l])], axis=-1)
        x = x + g_m * (y_mix @ w_out[l])

        h = rmsnorm(x, norm_ffn[l]) * (1.0 + sc_f) + sh_f
        gate, up = jnp.split(h @ w_up[l], 2, axis=-1)
        x = x + g_f * ((jax.nn.silu(gate) * up) @ w_down[l])

    return rmsnorm(x, norm_final)
```

```python
from collections import defaultdict
import numpy as np
import concourse.bass as bass
import concourse.mybir as mybir
from concourse.bass_utils import run_bass_kernel_spmd

F32 = mybir.dt.float32
BF16 = mybir.dt.bfloat16
AF = mybir.ActivationFunctionType
ALU = mybir.AluOpType

D = 1024
BATCH = 16
SEQ = 2048
DEPTH = 2
NCORE = 8
BPC = BATCH // NCORE
NTOK = BPC * SEQ
T = 1024
NS = 2
SUB = 512
NT = NTOK // T
TPB = SEQ // T
KC = 8
DFF = 2816
FC = DFF // 128
IN_COLS = 2560
EPS = 1e-6

PL = 84
O_NMIX, O_NFFN, O_CONV, O_ONA, O_ONB, O_BMOD = 0, 8, 16, 28, 32, 36
O_NFIN = DEPTH * PL
NCST = O_NFIN + 8

NSLOT = 4
SLOT_ELEMS = 5632

A_HC = 0
A_GU = 16384
A_VN = 32768
A_YMIX = 40960
A_ACT = 0
A_SG = 45056
NSG = 6
A_YOUT = 0
A_WMOD = 0
A_WSF = 49152
ARENA_BYTES = 57344


class Sched:
    def __init__(self, engs, sems, same_engine_sync=True):
        self.engs = engs
        self.sems = sems
        self.cnt = defaultdict(int)
        self.waited = defaultdict(dict)
        self.lastw = {}
        self.readers = defaultdict(dict)
        self.same = same_engine_sync

    def op(self, eng, fns, reads=(), writes=(), sem=None, inc=1, inc_each=False):
        if callable(fns):
            fns = [fns]
        if sem is None:
            sem = eng
        else:
            writes = list(writes) + [("__sem", sem)]
        deps = {}
        for k in reads:
            t = self.lastw.get(k)
            if t is not None:
                deps[t[0]] = max(deps.get(t[0], 0), t[1])
        for k in writes:
            t = self.lastw.get(k)
            if t is not None:
                deps[t[0]] = max(deps.get(t[0], 0), t[1])
            for s, v in self.readers[k].items():
                deps[s] = max(deps.get(s, 0), v)
        e = self.engs[eng]
        w = self.waited[eng]
        for s, v in deps.items():
            if s == eng and (eng == "pe" or not self.same):
                continue
            if w.get(s, 0) < v:
                e.wait_ge(self.sems[s], v)
                w[s] = v
        ins = None
        for f in fns:
            ins = f(e)
            if inc_each:
                ins.then_inc(self.sems[sem], inc)
                self.cnt[sem] += inc
        if not inc_each:
            ins.then_inc(self.sems[sem], inc)
            self.cnt[sem] += inc
        tok = (sem, self.cnt[sem])
        for k in reads:
            r = self.readers[k]
            r[sem] = max(r.get(sem, 0), tok[1])
        for k in writes:
            self.lastw[k] = tok
            self.readers[k] = {}
        return tok

    def wait_all(self, eng, sem_names):
        e = self.engs[eng]
        for s in sem_names:
            v = self.cnt[s]
            if v > 0 and self.waited[eng].get(s, 0) < v:
                e.wait_ge(self.sems[s], v)
                self.waited[eng][s] = v


class Ring:
    def __init__(self, n):
        self.n = n
        self.i = 0

    def next(self):
        r = self.i % self.n
        self.i += 1
        return r


def build_program(layers, final_norm, n_tiles=NT):
    nc = bass.Bass("TRN2", target_bir_lowering=False)
    dt_in = lambda name, shape: nc.dram_tensor(name, shape, F32, kind="ExternalInput").ap()
    xT_d = dt_in("xT", [128, KC, NTOK])
    cT_d = dt_in("cT", [128, KC, BPC])
    wmod_d = dt_in("w_mod", [DEPTH, D, 6 * D])
    win_d = dt_in("w_in", [DEPTH, D, IN_COLS])
    wout_d = dt_in("w_out", [DEPTH, D, D])
    wup_d = dt_in("w_up", [DEPTH, D, 2 * DFF])
    wdn_d = dt_in("w_down", [DEPTH, DFF, D])
    cst_d = dt_in("cst", [128, NCST])
    vnb_d = dt_in("vnb", [128, DEPTH, 512])
    bsT_d = dt_in("bsT", [128, DEPTH, 4, 128])
    wsT_d = dt_in("wsT", [DEPTH, 128, 8, 128])
    yT_d = nc.dram_tensor("yT", [128, KC, NTOK], F32, kind="ExternalOutput").ap()

    A = nc.alloc_sbuf_tensor
    xT = A("xT_sb", [128, KC, T], F32)
    hT = A("hT_sb", [128, KC, T], BF16)
    arena = A("arena", [128, ARENA_BYTES // 4], F32)
    zT = A("zT_sb", [128, 4, T + 8], F32)
    wring = A("wring", [128, NSLOT * SLOT_ELEMS], BF16)
    NSQ = 6
    sqr = A("sqr", [128, NSQ, SUB], BF16)
    NTMP = 3
    tmpr = A("tmpr", [128, NTMP, SUB], F32)
    NRS = 4
    rstd = A("rstd", [128, NRS, SUB], F32)
    cst = A("cst_sb", [128, NCST], F32)
    vnb = A("vnb_sb", [128, DEPTH, 512], F32)
    bsT = A("bsT_sb", [128, DEPTH, 4, 128], F32)
    wsb = A("wsb_sb", [128, DEPTH, 8, 128], BF16)
    modt = A("modt", [128, DEPTH, 48, BPC], F32)
    amod = A("amod", [128, DEPTH, 2, KC, BPC], F32)
    cact = A("cact", [128, KC, BPC], F32)
    ones = A("ones", [128, 128], BF16)
    epst = A("epst", [128, 1], F32)
    junk1 = A("junk1", [128, 1], F32)
    nhalf = A("nhalf", [128, 1], F32)
    mbuf = A("mbuf", [128, 2, 1024], F32)
    zsave = A("zsave", [128, DEPTH, 4, 2], F32)
    ssv = A("ssv", [128, 8], F32)
    rvv = A("rvv", [128, 8], F32)

    psum = [nc.alloc_psum_tensor(f"ps{i}", [128, SUB], F32) for i in range(8)]

    sem_names = ["pe", "act", "dve", "pool", "w0", "w1", "w2", "dx0", "dx1", "dy0", "dy1", "dc", "dm0", "dm1", "dm2", "w3"]
    sems = {n: nc.alloc_semaphore(n) for n in sem_names}
    engs = {"pe": nc.tensor, "act": nc.scalar, "dve": nc.vector, "pool": nc.gpsimd, "sp": nc.sync}
    S = Sched(engs, sems)

    def akeys(off, nbytes):
        return [("A", i) for i in range(off // 1024, (off + nbytes + 1023) // 1024)]

    def af32(off, n=SUB):
        return arena[:, off // 4: off // 4 + n]

    def abf(off, n=SUB):
        return arena[:, off // 4: off // 4 + n // 2].bitcast(BF16)

    def hc_off(oc, s):
        return A_HC + (oc * T + s * SUB) * 4

    def gu_off(oc, s):
        return A_GU + (oc * T + s * SUB) * 4

    def vn_off(tt):
        return A_VN + tt * 1024

    def ymix_off(k, s):
        return A_YMIX + (k * T + s * SUB) * 2

    def act_off(j, s):
        return A_ACT + (j * T + s * SUB) * 2

    def sg_off(i):
        return A_SG + i * 2048

    def yout_off(c, s):
        return A_YOUT + (c * T + s * SUB) * 4

    def cs(col):
        return cst[:, col:col + 1]

    def ps_key(b):
        return ("ps", b)

    sq_ring = Ring(NSQ)
    tmp_ring = Ring(NTMP)
    rs_ring = Ring(NRS)
    sg_ring = Ring(NSG)
    main_pair = Ring(2)
    main_single = Ring(5)
    SINGLE_BANKS = [0, 1, 2, 3, 6]
    stat_bank = Ring(2)
    misc_bank = Ring(1)

    fills = []
    for ti in range(n_tiles):
        for l in layers:
            for f in range(5):
                fills.append(("in", l, f))
            for f in range(2):
                fills.append(("out", l, f))
            for f in range(11):
                fills.append(("up", l, f))
            for f in range(4):
                fills.append(("down", l, f))
    wstate = {"issued": 0, "cur": 0, "oldest": 0}
    IN_ORDER = [4, 3, 2, 1, 0]

    def slot_view(slot, k, n):
        return wring[:, slot * SLOT_ELEMS: slot * SLOT_ELEMS + k * n].rearrange("p (k n) -> p k n", k=k)

    def issue_fill(i):
        kind, l, f = fills[i]
        slot = i % NSLOT
        if kind == "in":
            g = IN_ORDER[f]
            src = win_d[l].rearrange("(k p) n -> p k n", p=128)[:, :, g * 512:(g + 1) * 512]
            fns = [lambda e, src=src: e.dma_start(out=slot_view(slot, 8, 512), in_=src)]
        elif kind == "out":
            src = wout_d[l].rearrange("(k p) n -> p k n", p=128)[:, :, f * 512:(f + 1) * 512]
            fns = [lambda e, src=src: e.dma_start(out=slot_view(slot, 8, 512), in_=src)]
        elif kind == "up":
            wv = wup_d[l].rearrange("(k p) n -> p k n", p=128)
            sg_ = wv[:, :, f * 256:(f + 1) * 256]
            su_ = wv[:, :, DFF + f * 256: DFF + (f + 1) * 256]
            dst = slot_view(slot, 8, 512)
            fns = [lambda e, sg_=sg_, dst=dst: e.dma_start(out=dst[:, :, 0:256], in_=sg_),
                   lambda e, su_=su_, dst=dst: e.dma_start(out=dst[:, :, 256:512], in_=su_)]
        else:
            src = wdn_d[l].rearrange("(k p) n -> p k n", p=128)[:, :, f * 256:(f + 1) * 256]
            fns = [lambda e, src=src: e.dma_start(out=slot_view(slot, FC, 256), in_=src)]
        S.op("pool", fns, writes=[("w", slot)], sem=f"w{slot}", inc=16, inc_each=True)

    def acquire(kind, l, f, hold=False):
        i = wstate["cur"]
        assert fills[i] == (kind, l, f), (fills[i], kind, l, f)
        if not hold:
            wstate["oldest"] = i
        while wstate["issued"] < len(fills) and wstate["issued"] <= wstate["oldest"] + NSLOT - 1:
            issue_fill(wstate["issued"])
            wstate["issued"] += 1
        wstate["cur"] += 1
        return i % NSLOT

    deferred = []

    def flush_deferred():
        for d in deferred:
            d()
        deferred.clear()

    mod_pending = []
    mring = Ring(2)
    MODBANK = 7

    def mod_unit():
        if not mod_pending:
            return
        l, oc = mod_pending.pop(0)
        bi = mring.next()
        mv = mbuf[:, bi, :].rearrange("p (k n) -> p k n", k=8)
        src = wmod_d[l].rearrange("(k p) n -> p k n", p=128)[:, :, oc * 128:(oc + 1) * 128]
        S.op("sp", lambda e: e.dma_start(out=mv, in_=src), writes=[("mbuf", bi)], sem=f"dm{1 + bi}", inc=16)
        fns = [lambda e, k=k: e.matmul(psum[MODBANK][:, 0:BPC], mv[:, k, :], cact[:, k, :],
                                       start=(k == 0), stop=(k == KC - 1)) for k in range(KC)]
        S.op("pe", fns, reads=[("mbuf", bi), "cact"], writes=[ps_key(MODBANK)])
        S.op("dve", lambda e: e.tensor_scalar(out=modt[:, l, oc, :], in0=psum[MODBANK][:, 0:BPC],
                                              scalar1=cs(l * PL + O_BMOD + oc), scalar2=None, op0=ALU.add),
             reads=[ps_key(MODBANK), "cst"], writes=[("modc", l, oc)])
        if oc % 8 == 7 and oc // 8 in (1, 4):
            make_amod(l, 0 if oc // 8 == 1 else 1)

    def make_amod(l, which):
        m_sc, o_n = ((1, O_NMIX), (4, O_NFFN))[which]
        base = l * PL
        for b in range(BPC):
            S.op("dve", lambda e, b=b: e.scalar_tensor_tensor(
                out=amod[:, l, which, :, b], in0=modt[:, l, m_sc * 8:(m_sc + 1) * 8, b], scalar=1.0,
                in1=cst[:, base + o_n: base + o_n + 8], op0=ALU.add, op1=ALU.mult),
                reads=[("modc", l, m_sc * 8 + c) for c in range(KC)] + ["cst"],
                writes=[("amod", l, which, b)])

    def proj_group(slot, wsel, K, rhs_fn, rhs_keys):
        pr = main_pair.next()
        banks = (2 * pr, 2 * pr + 1)
        fns = []
        for k in range(K):
            for s in range(NS):
                fns.append(lambda e, k=k, s=s: e.matmul(psum[banks[s]][:], wsel(k), rhs_fn(k, s),
                                                        start=(k == 0), stop=(k == K - 1)))
        S.op("pe", fns, reads=[("w", slot)] + rhs_keys, writes=[ps_key(banks[0]), ps_key(banks[1])])
        mod_unit()
        return banks

    def proj_single(slot, wsel, K, rhs_fn, rhs_keys):
        bank = SINGLE_BANKS[main_single.next()]
        fns = [lambda e, k=k: e.matmul(psum[bank][:], wsel(k), rhs_fn(k), start=(k == 0), stop=(k == K - 1))
               for k in range(K)]
        S.op("pe", fns, reads=[("w", slot)] + rhs_keys, writes=[ps_key(bank)])
        mod_unit()
        return bank

    def stat_matmul(bank, src_ap, src_keys, first, last):
        S.op("pe", lambda e: e.matmul(psum[bank][:], ones[:], src_ap, start=first, stop=last),
             reads=src_keys + ["ones"], writes=[ps_key(bank)])

    def square_to_ring(src_ap, src_keys):
        i = sq_ring.next()
        S.op("act", lambda e: e.activation(out=sqr[:, i, :], in_=src_ap, func=AF.Square),
             reads=src_keys, writes=[("sq", i)])
        return sqr[:, i, :], [("sq", i)]

    def make_rstd(bank, inv_n):
        i = rs_ring.next()
        S.op("act", lambda e: e.activation(out=rstd[:, i, :], in_=psum[bank][:], func=AF.Ln,
                                           bias=epst[:, 0:1], scale=inv_n),
             reads=[ps_key(bank), "eps"], writes=[("rs", i)])
        S.op("act", lambda e: e.activation(out=rstd[:, i, :], in_=rstd[:, i, :], func=AF.Exp, scale=-0.5),
             reads=[("rs", i)], writes=[("rs", i)])
        return i

    def preload_sqrt_table():
        S.op("act", lambda e: e.activation(out=junk1[:, 0:1], in_=epst[:, 0:1], func=AF.Ln),
             reads=["eps"], writes=["junk1"])
        S.op("act", lambda e: e.activation(out=junk1[:, 0:1], in_=junk1[:, 0:1], func=AF.Exp),
             reads=["junk1"], writes=["junk1"])

    def xs(c, s):
        return xT[:, c, s * SUB:(s + 1) * SUB]

    def x_stats_s(s):
        bank = 4 + stat_bank.next()
        for c in range(KC):
            sap, sk = square_to_ring(xs(c, s), [("x", c, s)])
            stat_matmul(bank, sap, sk, c == 0, c == KC - 1)
        return make_rstd(bank, 1.0 / D)

    def norm_half(l, b, which, s, r):
        sh_m = 0 if which == 0 else 3
        assert ("amod", l, which, b) in S.lastw and ("modc", l, sh_m * 8 + KC - 1) in S.lastw
        for c in range(KC):
            ti_ = tmp_ring.next()
            S.op("dve", lambda e, c=c, ti_=ti_: e.tensor_tensor(
                out=tmpr[:, ti_, :], in0=xs(c, s), in1=rstd[:, r, :], op=ALU.mult),
                reads=[("x", c, s), ("rs", r)], writes=[("tmp", ti_)])
            S.op("act", lambda e, c=c, ti_=ti_: e.activation(
                out=hT[:, c, s * SUB:(s + 1) * SUB], in_=tmpr[:, ti_, :], func=AF.Identity,
                bias=modt[:, l, sh_m * 8 + c, b:b + 1], scale=amod[:, l, which, c, b:b + 1]),
                reads=[("tmp", ti_), ("modc", l, sh_m * 8 + c), ("amod", l, which, b)], writes=[("h", c, s)])

    def x_update_s(l, b, m_g, oc, s, bank, sbank_x):
        assert ("modc", l, m_g * 8 + oc) in S.lastw
        S.op("dve", lambda e: e.scalar_tensor_tensor(
            out=xs(oc, s), in0=psum[bank][:], scalar=modt[:, l, m_g * 8 + oc, b:b + 1], in1=xs(oc, s),
            op0=ALU.mult, op1=ALU.add),
            reads=[ps_key(bank), ("x", oc, s), ("modc", l, m_g * 8 + oc)], writes=[("x", oc, s)])
        sap, sk = square_to_ring(xs(oc, s), [("x", oc, s)])
        deferred.append(lambda: stat_matmul(sbank_x[s], sap, sk, oc == 0, oc == KC - 1))

    def residual_proj(kind, l, b, m_g, nfills, cpf, K, rhs_fn, rhs_keys_fn, next_norm, all_souter):
        sbank_x = [4 + stat_bank.next() for _ in range(NS)]
        for f in range(nfills):
            slot = acquire(kind, l, f)
            wv = slot_view(slot, K, cpf * 128)
            last = f == nfills - 1
            if last or all_souter:
                for s in range(NS):
                    for ocl in range(cpf):
                        oc = f * cpf + ocl
                        bank = proj_single(slot, lambda k, ocl=ocl: wv[:, k, ocl * 128:(ocl + 1) * 128], K,
                                           lambda k, s=s: rhs_fn(k, s), rhs_keys_fn(s))
                        flush_deferred()
                        if last and s == 1 and ocl == 0:
                            next_norm(0, make_rstd(sbank_x[0], 1.0 / D))
                        x_update_s(l, b, m_g, oc, s, bank, sbank_x)
            else:
                for ocl in range(cpf):
                    oc = f * cpf + ocl
                    banks = proj_group(slot, lambda k, ocl=ocl: wv[:, k, ocl * 128:(ocl + 1) * 128], K,
                                       rhs_fn, rhs_keys_fn(0) + rhs_keys_fn(1))
                    flush_deferred()
                    for s in range(NS):
                        x_update_s(l, b, m_g, oc, s, banks[s], sbank_x)
        flush_deferred()
        r1 = make_rstd(sbank_x[1], 1.0 / D)
        late.append(lambda: next_norm(1, r1))

    late = []

    def run_late():
        while late:
            late.pop(0)()

    def h_rhs(k, s):
        return hT[:, k, s * SUB:(s + 1) * SUB]

    def h_keys():
        return [("h", k, s) for k in range(KC) for s in range(NS)]

    def load_x(ti, s):
        S.op("sp", [lambda e, c=c: e.dma_start(out=xs(c, s), in_=xT_d[:, c, ti * T + s * SUB: ti * T + (s + 1) * SUB])
                    for c in range(KC)],
             writes=[("x", c, s) for c in range(KC)], sem=f"dx{s}", inc=16, inc_each=True)

    S.op("sp", [lambda e: e.dma_start(out=cst[:], in_=cst_d),
                lambda e: e.dma_start(out=cact[:], in_=cT_d),
                lambda e: e.dma_start(out=vnb[:], in_=vnb_d),
                lambda e: e.dma_start(out=bsT[:], in_=bsT_d)],
         writes=["cst", "cact", "vnb", "bsT"], sem="dc", inc=16, inc_each=True)
    for s in range(NS):
        load_x(0, s)
    S.op("dve", lambda e: e.memset(epst[:], EPS), writes=["eps"])
    S.op("dve", lambda e: e.memset(nhalf[:], -0.5), writes=["nhalf"])
    S.op("dve", lambda e: e.memset(ones[:], 1.0), writes=["ones"])
    S.op("act", lambda e: e.activation(out=cact[:], in_=cact[:], func=AF.Silu), reads=["cact"], writes=["cact"])
    for l in layers:
        wsf = arena[:, A_WSF // 4: A_WSF // 4 + 1024].rearrange("p (h i) -> p h i", h=8)
        S.op("sp", lambda e, l=l: e.dma_start(out=wsf, in_=wsT_d[l]), writes=akeys(A_WSF, 4096), sem="dc", inc=16)
        S.op("pool", lambda e, l=l: e.affine_select(out=wsb[:, l, :, :], in_=wsf, pattern=[[0, 8], [1, 128]],
                                                    compare_op=ALU.is_ge, fill=0.0, base=0, channel_multiplier=-1),
             reads=akeys(A_WSF, 4096), writes=[("wsb", l)])
    rs0 = [x_stats_s(s) for s in range(NS)]
    dm_ring = Ring(3)
    l0 = layers[0]
    up_list = [(l0, slab) for slab in range(4)] if not DBG_MOD_UPFRONT else [(l, slab) for l in layers for slab in range(12)]
    for l0, slab in up_list:
        di = dm_ring.next()
        off = A_WMOD + di * 16384
        sv = arena[:, off // 4: off // 4 + 4096].rearrange("p (k n) -> p k n", k=8)
        src = wmod_d[l0].rearrange("(k p) n -> p k n", p=128)[:, :, slab * 512:(slab + 1) * 512]
        S.op("sp", lambda e, sv=sv, src=src: e.dma_start(out=sv, in_=src),
             writes=akeys(off, 16384), sem=f"dm{di}", inc=16)
        fns = []
        for j in range(4):
            for k in range(KC):
                fns.append(lambda e, sv=sv, j=j, k=k: e.matmul(
                    psum[MODBANK][:, j * BPC:(j + 1) * BPC], sv[:, k, j * 128:(j + 1) * 128], cact[:, k, :],
                    start=(k == 0), stop=(k == KC - 1)))
        S.op("pe", fns, reads=akeys(off, 16384) + ["cact"], writes=[ps_key(MODBANK)])
        cb = l0 * PL + O_BMOD + slab * 4
        S.op("dve", lambda e, slab=slab, cb=cb: e.tensor_tensor(
            out=modt[:, l0, slab * 4:(slab + 1) * 4, :],
            in0=psum[MODBANK][:, 0:4 * BPC].rearrange("p (o b) -> p o b", b=BPC),
            in1=cst[:, cb: cb + 4].unsqueeze(2).to_broadcast([128, 4, BPC]), op=ALU.add),
            reads=[ps_key(MODBANK), "cst"], writes=[("modc", l0, slab * 4 + j) for j in range(4)])
    l0 = layers[0]
    make_amod(l0, 0)
    if DBG_MOD_UPFRONT:
        make_amod(l0, 1)
        for l in layers[1:]:
            make_amod(l, 0)
            make_amod(l, 1)
    else:
        for l in layers:
            for oc in range(48):
                if l == l0 and oc < 16:
                    continue
                mod_pending.append((l, oc))

    def mixer(l, ti, next_norm):
        b = ti // TPB
        half = ti % TPB
        base = l * PL
        hk = h_keys()

        def v_group(slot, wv, tt):
            bank = SINGLE_BANKS[main_single.next()]
            s_ = tt // 4
            fns = [lambda e, k=k: e.matmul(psum[bank][:], hT[:, k, tt * 128:(tt + 1) * 128], wv[:, k, :],
                                           start=(k == 0), stop=(k == KC - 1)) for k in range(KC)]
            S.op("pe", fns, reads=[("w", slot)] + [("h", k, s_) for k in range(KC)], writes=[ps_key(bank)])
            mod_unit()
            ti_ = tmp_ring.next()
            S.op("act", lambda e: e.activation(out=tmpr[:, ti_, :], in_=psum[bank][:], func=AF.Gelu_apprx_tanh),
                 reads=[ps_key(bank)], writes=[("tmp", ti_)])
            qi = sq_ring.next()
            S.op("act", lambda e: e.activation(out=sqr[:, qi, :], in_=tmpr[:, ti_, :],
                                               func=AF.Square, accum_out=ssv[:, tt:tt + 1]),
                 reads=[("tmp", ti_)], writes=[("sq", qi), ("ss", tt)])
            S.op("dve", lambda e: e.tensor_scalar(out=rvv[:, tt:tt + 1], in0=ssv[:, tt:tt + 1],
                                                  scalar1=1.0 / 512, scalar2=EPS, op0=ALU.mult, op1=ALU.add),
                 reads=[("ss", tt)], writes=[("rv", tt)])
            S.op("pool", lambda e: e.tensor_tensor(out=rvv[:, tt:tt + 1], in0=rvv[:, tt:tt + 1],
                                                   in1=nhalf[:, 0:1], op=ALU.pow),
                 reads=[("rv", tt), "nhalf"], writes=[("rv", tt)])
            S.op("dve", lambda e: e.scalar_tensor_tensor(
                out=abf(vn_off(tt)), in0=tmpr[:, ti_, :], scalar=rvv[:, tt:tt + 1], in1=vnb[:, l, :],
                op0=ALU.mult, op1=ALU.mult),
                reads=[("tmp", ti_), ("rv", tt), "vnb"], writes=akeys(vn_off(tt), 1024))

        slot_v = acquire("in", l, 0)
        wv_v = slot_view(slot_v, 8, 512)
        slot_u = acquire("in", l, 1, hold=True)
        wv_u = slot_view(slot_u, 8, 512)
        for s in range(NS):
            for ttl in range(4):
                v_group(slot_v, wv_v, s * 4 + ttl)
                if s == 0 and ttl == 1:
                    run_late()
            hks = [("h", k, s) for k in range(KC)]
            for oc in range(4):
                bank = proj_single(slot_u, lambda k, oc=oc: wv_u[:, k, oc * 128:(oc + 1) * 128], KC,
                                   lambda k, s=s: h_rhs(k, s), hks)
                S.op("act", lambda e, oc=oc, s=s, bank=bank: e.activation(
                    out=af32(gu_off(oc, s)), in_=psum[bank][:], func=AF.Gelu_apprx_tanh),
                    reads=[ps_key(bank)], writes=akeys(gu_off(oc, s), 2048))

        preload_sqrt_table()
        sbank_b = [4 + stat_bank.next() for _ in range(NS)]
        sgu_list = [(s, fc) for s in range(NS) for fc in range(4)]

        def sgu_group():
            if not sgu_list:
                return
            s, fc = sgu_list.pop(0)
            bank = 6 + misc_bank.next()
            fns = []
            rk = [("wsb", l)]
            for ttl in range(4):
                tt = s * 4 + ttl
                rk += akeys(vn_off(tt), 1024)
                for hh in range(2):
                    h = 2 * fc + hh
                    fns.append(lambda e, ttl=ttl, tt=tt, hh=hh, h=h: e.matmul(
                        psum[bank][hh * 64:(hh + 1) * 64, ttl * 128:(ttl + 1) * 128],
                        abf(vn_off(tt))[:, h * 64:(h + 1) * 64], wsb[:, l, h, :], start=True, stop=True))
            S.op("pe", fns, reads=rk, writes=[ps_key(bank)])
            flush_deferred()
            ti_ = tmp_ring.next()
            S.op("dve", lambda e: e.tensor_tensor(
                out=tmpr[:, ti_, :].rearrange("p (a i) -> p a i", a=4),
                in0=psum[bank][:].rearrange("p (a i) -> p a i", a=4),
                in1=bsT[:, l, fc, :].unsqueeze(1).to_broadcast([128, 4, 128]), op=ALU.add),
                reads=[ps_key(bank), "bsT"], writes=[("tmp", ti_)])
            gk = akeys(gu_off(fc, s), 2048)
            S.op("pool", lambda e: e.tensor_tensor(
                out=af32(gu_off(fc, s)), in0=tmpr[:, ti_, :], in1=af32(gu_off(fc, s)), op=ALU.mult),
                reads=[("tmp", ti_)] + gk, writes=gk)
            sap, sk = square_to_ring(af32(gu_off(fc, s)), gk)
            deferred.append(lambda: stat_matmul(sbank_b[s], sap, sk, fc == 0, fc == 3))

        slot = acquire("in", l, 2)
        wv = slot_view(slot, 8, 512)
        for oc in range(4):
            banks = proj_group(slot, lambda k, oc=oc: wv[:, k, oc * 128:(oc + 1) * 128], KC, h_rhs, hk)
            for s in range(NS):
                S.op("act", lambda e, oc=oc, s=s: e.activation(out=af32(hc_off(oc, s)), in_=psum[banks[s]][:],
                                                               func=AF.Identity),
                     reads=[ps_key(banks[s])], writes=akeys(hc_off(oc, s), 2048))
            sgu_group()
        if half == 0:
            S.op("pool", lambda e: e.memset(zT[:, :, 0:2], 0.0), writes=[("zpad",)])
        else:
            S.op("pool", lambda e: e.tensor_copy(out=zT[:, :, 0:2], in_=zsave[:, l, :, :]),
                 reads=[("zsave", l)], writes=[("zpad",)])
        slot = acquire("in", l, 3)
        wv = slot_view(slot, 8, 512)
        for oc in range(4):
            banks = proj_group(slot, lambda k, oc=oc: wv[:, k, oc * 128:(oc + 1) * 128], KC, h_rhs, hk)
            flush_deferred()
            for s in range(NS):
                S.op("dve", lambda e, oc=oc, s=s: e.tensor_tensor(
                    out=zT[:, oc, 2 + s * SUB: 2 + (s + 1) * SUB], in0=psum[banks[s]][:], in1=af32(hc_off(oc, s)),
                    op=ALU.mult),
                    reads=[ps_key(banks[s])] + akeys(hc_off(oc, s), 2048), writes=[("z", oc, s)])
            for s in range(NS):
                zk = [("z", oc, s), ("z", oc, s - 1) if s > 0 else ("zpad",)]
                ck = akeys(hc_off(oc, s), 2048)
                S.op("act", lambda e, oc=oc, s=s: e.activation(
                    out=af32(hc_off(oc, s)), in_=zT[:, oc, s * SUB: (s + 1) * SUB], func=AF.Identity,
                    scale=cs(base + O_CONV + 0 * 4 + oc)),
                    reads=zk + ["cst"], writes=ck)
                for tap in (1, 2):
                    S.op("dve", lambda e, oc=oc, s=s, tap=tap: e.scalar_tensor_tensor(
                        out=af32(hc_off(oc, s)), in0=zT[:, oc, tap + s * SUB: tap + (s + 1) * SUB],
                        scalar=cs(base + O_CONV + tap * 4 + oc), in1=af32(hc_off(oc, s)),
                        op0=ALU.mult, op1=ALU.add),
                        reads=zk + ck + ["cst"], writes=ck)
            sgu_group()
        if half < TPB - 1:
            S.op("pool", lambda e: e.tensor_copy(out=zsave[:, l, :, :], in_=zT[:, :, T:T + 2]),
                 reads=[("z", oc, NS - 1) for oc in range(4)], writes=[("zsave", l)])
        assert not sgu_list
        slot = acquire("in", l, 4)
        wv = slot_view(slot, 8, 512)
        sbank_a = None
        for oc in range(4):
            banks = proj_group(slot, lambda k, oc=oc: wv[:, k, oc * 128:(oc + 1) * 128], KC, h_rhs, hk)
            flush_deferred()
            if oc == 0:
                for s2 in range(NS):
                    r_b = make_rstd(sbank_b[s2], 1.0 / 512)
                    for fc in range(4):
                        S.op("dve", lambda e, fc=fc, s2=s2, r_b=r_b: e.scalar_tensor_tensor(
                            out=abf(ymix_off(4 + fc, s2)), in0=af32(gu_off(fc, s2)), scalar=cs(base + O_ONB + fc),
                            in1=rstd[:, r_b, :], op0=ALU.mult, op1=ALU.mult),
                            reads=akeys(gu_off(fc, s2), 2048) + [("rs", r_b), "cst"],
                            writes=akeys(ymix_off(4 + fc, s2), 1024))
                sbank_a = [4 + stat_bank.next() for _ in range(NS)]
            for s in range(NS):
                ck = akeys(hc_off(oc, s), 2048)
                S.op("dve", lambda e, oc=oc, s=s: e.tensor_tensor(
                    out=af32(hc_off(oc, s)), in0=psum[banks[s]][:], in1=af32(hc_off(oc, s)), op=ALU.mult),
                    reads=[ps_key(banks[s])] + ck, writes=ck)
                sap, sk = square_to_ring(af32(hc_off(oc, s)), ck)
                deferred.append(lambda s=s, sap=sap, sk=sk, oc=oc: stat_matmul(sbank_a[s], sap, sk, oc == 0, oc == 3))
        def ya_norm():
            flush_deferred()
            for s2 in range(NS):
                r_a = make_rstd(sbank_a[s2], 1.0 / 512)
                for oc in range(4):
                    S.op("dve", lambda e, oc=oc, s2=s2, r_a=r_a: e.scalar_tensor_tensor(
                        out=abf(ymix_off(oc, s2)), in0=af32(hc_off(oc, s2)), scalar=cs(base + O_ONA + oc),
                        in1=rstd[:, r_a, :], op0=ALU.mult, op1=ALU.mult),
                        reads=akeys(hc_off(oc, s2), 2048) + [("rs", r_a), "cst"],
                        writes=akeys(ymix_off(oc, s2), 1024))
        ya_norm()
        residual_proj("out", l, b, 2, 2, 4, KC, lambda k, s: abf(ymix_off(k, s)),
                      lambda s: [k_ for k in range(KC) for k_ in akeys(ymix_off(k, s), 1024)], next_norm, True)

    def ffn(l, ti, next_norm):
        b = ti // TPB
        hk = h_keys()
        slots = [acquire("up", l, 0), acquire("up", l, 1, hold=True)]
        for s in range(NS):
            hks = [("h", k, s) for k in range(KC)]
            for f in range(2):
                wv = slot_view(slots[f], 8, 512)
                for jl in range(2):
                    j = 2 * f + jl
                    bank = proj_single(slots[f], lambda k, jl=jl, wv=wv: wv[:, k, jl * 128:(jl + 1) * 128], KC,
                                       lambda k, s=s: h_rhs(k, s), hks)
                    gi = sg_ring.next()
                    S.op("act", lambda e, gi=gi, bank=bank: e.activation(
                        out=af32(sg_off(gi)), in_=psum[bank][:], func=AF.Silu),
                        reads=[ps_key(bank)], writes=akeys(sg_off(gi), 2048))
                    bank = proj_single(slots[f], lambda k, jl=jl, wv=wv: wv[:, k, 256 + jl * 128: 256 + (jl + 1) * 128],
                                       KC, lambda k, s=s: h_rhs(k, s), hks)
                    S.op("dve", lambda e, s=s, j=j, gi=gi, bank=bank: e.tensor_tensor(
                        out=abf(act_off(j, s)), in0=psum[bank][:], in1=af32(sg_off(gi)), op=ALU.mult),
                        reads=[ps_key(bank)] + akeys(sg_off(gi), 2048), writes=akeys(act_off(j, s), 1024))
                    if s == 0 and f == 0 and jl == 0:
                        run_late()
        for f in range(2, 11):
            slot = acquire("up", l, f)
            wv = slot_view(slot, 8, 512)
            for jl in range(2):
                j = 2 * f + jl
                banks = proj_group(slot, lambda k, jl=jl: wv[:, k, jl * 128:(jl + 1) * 128], KC, h_rhs, hk)
                sgi = []
                for s in range(NS):
                    gi = sg_ring.next()
                    sgi.append(gi)
                    S.op("act", lambda e, s=s, gi=gi: e.activation(out=af32(sg_off(gi)), in_=psum[banks[s]][:],
                                                                   func=AF.Silu),
                         reads=[ps_key(banks[s])], writes=akeys(sg_off(gi), 2048))
                banks = proj_group(slot, lambda k, jl=jl: wv[:, k, 256 + jl * 128: 256 + (jl + 1) * 128], KC, h_rhs, hk)
                for s in range(NS):
                    S.op("dve", lambda e, s=s, j=j, gi=sgi[s]: e.tensor_tensor(
                        out=abf(act_off(j, s)), in0=psum[banks[s]][:], in1=af32(sg_off(gi)), op=ALU.mult),
                        reads=[ps_key(banks[s])] + akeys(sg_off(sgi[s]), 2048), writes=akeys(act_off(j, s), 1024))
        preload_sqrt_table()
        residual_proj("down", l, b, 5, 4, 2, FC, lambda k, s: abf(act_off(k, s)),
                      lambda s: [k_ for j in range(FC) for k_ in akeys(act_off(j, s), 1024)], next_norm, False)

    def final_half(ti, s, r):
        if final_norm:
            hT32 = hT[:, :, :].rearrange("p k t -> p (k t)").bitcast(F32)
            def yo(c):
                return hT32[:, c * SUB:(c + 1) * SUB] if s == 0 else af32(yout_off(c, 1))
            def yk(c):
                return [("h", c, 0), ("h", c, 1)] if s == 0 else akeys(yout_off(c, 1), 2048)
            for c in range(KC):
                S.op("dve", lambda e, c=c: e.scalar_tensor_tensor(
                    out=yo(c), in0=xs(c, s), scalar=cs(O_NFIN + c),
                    in1=rstd[:, r, :], op0=ALU.mult, op1=ALU.mult),
                    reads=[("x", c, s), ("rs", r), "cst"], writes=yk(c))
            S.op("sp", [lambda e, c=c: e.dma_start(out=yT_d[:, c, ti * T + s * SUB: ti * T + (s + 1) * SUB], in_=yo(c))
                        for c in range(KC)],
                 reads=[k_ for c in range(KC) for k_ in yk(c)],
                 sem=f"dy{s}", inc=16, inc_each=True)
        else:
            S.op("sp", [lambda e, c=c: e.dma_start(out=yT_d[:, c, ti * T + s * SUB: ti * T + (s + 1) * SUB],
                                                   in_=xs(c, s))
                        for c in range(KC)],
                 reads=[("x", c, s) for c in range(KC)], sem=f"dy{s}", inc=16, inc_each=True)
        if ti + 1 < n_tiles:
            load_x(ti + 1, s)

    for ti in range(n_tiles):
        b = ti // TPB
        for s in range(NS):
            r = rs0[s] if ti == 0 else x_stats_s(s)
            norm_half(layers[0], b, 0, s, r)
        for li, l in enumerate(layers):
            mixer(l, ti, lambda s, r, l=l: norm_half(l, b, 1, s, r))
            if li + 1 < len(layers):
                nn = lambda s, r, l2=layers[li + 1]: norm_half(l2, b, 0, s, r)
            else:
                nn = lambda s, r: final_half(ti, s, r)
            ffn(l, ti, nn)
        run_late()
    S.wait_all("sp", ["dy0", "dy1"])
    return nc


def _prep_inputs(x, c, w_mod, b_mod, norm_mix, w_in, conv_w, v_norm, w_s, b_s, out_norm_a, out_norm_b,
                 w_out, norm_ffn, w_up, w_down, norm_final):
    f = lambda a: np.ascontiguousarray(np.asarray(a, dtype=np.float32))
    x, c = f(x), f(c)
    pp = lambda v: np.asarray(v, np.float32).reshape(-1, 128).T
    cst = np.zeros((128, NCST), np.float32)
    for l in range(DEPTH):
        base = l * PL
        cst[:, base + O_NMIX: base + O_NMIX + 8] = pp(norm_mix[l])
        cst[:, base + O_NFFN: base + O_NFFN + 8] = pp(norm_ffn[l])
        for k in range(3):
            cst[:, base + O_CONV + 4 * k: base + O_CONV + 4 * k + 4] = pp(conv_w[l][k])
        cst[:, base + O_ONA: base + O_ONA + 4] = pp(out_norm_a[l])
        cst[:, base + O_ONB: base + O_ONB + 4] = pp(out_norm_b[l])
        cst[:, base + O_BMOD: base + O_BMOD + 48] = pp(b_mod[l])
    cst[:, O_NFIN: O_NFIN + 8] = pp(norm_final)
    vnb = np.ascontiguousarray(np.broadcast_to(np.asarray(v_norm, np.float32)[None, :, :], (128, DEPTH, 512)))
    bs = np.asarray(b_s, np.float32)
    bsT = np.repeat(bs.reshape(DEPTH, 4, 2, 1, 128), 64, axis=3).reshape(DEPTH, 4, 128, 128)
    bsT = np.ascontiguousarray(bsT.transpose(2, 0, 1, 3))
    wsT = np.ascontiguousarray(np.asarray(w_s, np.float32).transpose(0, 3, 1, 2))
    shared = {"w_mod": f(w_mod), "w_in": f(w_in), "w_out": f(w_out), "w_up": f(w_up), "w_down": f(w_down),
              "cst": cst, "vnb": vnb, "bsT": bsT, "wsT": wsT}
    in_maps = []
    for i in range(NCORE):
        xb = x[i * BPC:(i + 1) * BPC].reshape(NTOK, KC, 128)
        xT = np.ascontiguousarray(xb.transpose(2, 1, 0))
        cT = np.ascontiguousarray(c[i * BPC:(i + 1) * BPC].reshape(BPC, KC, 128).transpose(2, 1, 0))
        m = dict(shared)
        m["xT"] = xT
        m["cT"] = cT
        in_maps.append(m)
    return in_maps


def _gather(results):
    out = np.empty((BATCH, SEQ, D), np.float32)
    for i, r in enumerate(results):
        yT = np.asarray(r["yT"])
        out[i * BPC:(i + 1) * BPC] = yT.transpose(2, 1, 0).reshape(BPC, SEQ, D)
    return out


FUSED = True
DBG_MOD_UPFRONT = False
_cache = {}


def _prog(layers, final_norm):
    key = (tuple(layers), final_norm)
    if key not in _cache:
        _cache[key] = build_program(list(layers), final_norm)
    return _cache[key]


def kernel(**inputs):
    in_maps = _prep_inputs(**inputs)
    cores = list(range(NCORE))
    if FUSED:
        res = run_bass_kernel_spmd(_prog(range(DEPTH), True), in_maps, core_ids=cores)
        return _gather(res.results)
    for l in range(DEPTH):
        last = l == DEPTH - 1
        res = run_bass_kernel_spmd(_prog([l], last), in_maps, core_ids=cores)
        if not last:
            for m, r in zip(in_maps, res.results):
                m["xT"] = np.ascontiguousarray(r["yT"])
    return _gather(res.results)
```

```python
from collections import defaultdict
import numpy as np
import concourse.bass as bass
import concourse.mybir as mybir
from concourse.bass_utils import run_bass_kernel_spmd

F32 = mybir.dt.float32
BF16 = mybir.dt.bfloat16
AF = mybir.ActivationFunctionType
ALU = mybir.AluOpType

D = 1024
BATCH = 16
SEQ = 2048
DEPTH = 2
NCORE = 8
BPC = BATCH // NCORE
NTOK = BPC * SEQ
T = 1024
NS = 2
SUB = 512
NT = NTOK // T
TPB = SEQ // T
KC = 8
DFF = 2816
FC = DFF // 128
IN_COLS = 2560
EPS = 1e-6

PL = 84
O_NMIX, O_NFFN, O_CONV, O_ONA, O_ONB, O_BMOD = 0, 8, 16, 28, 32, 36
O_NFIN = DEPTH * PL
NCST = O_NFIN + 8

NSLOT = 4
SLOT_ELEMS = 5632

A_HC = 0
A_GU = 16384
A_VN = 32768
A_YMIX = 40960
A_ACT = 0
A_SG = 45056
NSG = 6
A_YOUT = 0
A_WMOD = 0
A_WSF = 49152
ARENA_BYTES = 57344


class Sched:
    def __init__(self, engs, sems, same_engine_sync=True):
        self.engs = engs
        self.sems = sems
        self.cnt = defaultdict(int)
        self.waited = defaultdict(dict)
        self.lastw = {}
        self.readers = defaultdict(dict)
        self.same = same_engine_sync

    def op(self, eng, fns, reads=(), writes=(), sem=None, inc=1, inc_each=False):
        if callable(fns):
            fns = [fns]
        if sem is None:
            sem = eng
        else:
            writes = list(writes) + [("__sem", sem)]
        deps = {}
        for k in reads:
            t = self.lastw.get(k)
            if t is not None:
                deps[t[0]] = max(deps.get(t[0], 0), t[1])
        for k in writes:
            t = self.lastw.get(k)
            if t is not None:
                deps[t[0]] = max(deps.get(t[0], 0), t[1])
            for s, v in self.readers[k].items():
                deps[s] = max(deps.get(s, 0), v)
        e = self.engs[eng]
        w = self.waited[eng]
        for s, v in deps.items():
            if s == eng and (eng == "pe" or not self.same):
                continue
            if w.get(s, 0) < v:
                e.wait_ge(self.sems[s], v)
                w[s] = v
        ins = None
        for f in fns:
            ins = f(e)
            if inc_each:
                ins.then_inc(self.sems[sem], inc)
                self.cnt[sem] += inc
        if not inc_each:
            ins.then_inc(self.sems[sem], inc)
            self.cnt[sem] += inc
        tok = (sem, self.cnt[sem])
        for k in reads:
            r = self.readers[k]
            r[sem] = max(r.get(sem, 0), tok[1])
        for k in writes:
            self.lastw[k] = tok
            self.readers[k] = {}
        return tok

    def wait_all(self, eng, sem_names):
        e = self.engs[eng]
        for s in sem_names:
            v = self.cnt[s]
            if v > 0 and self.waited[eng].get(s, 0) < v:
                e.wait_ge(self.sems[s], v)
                self.waited[eng][s] = v


class Ring:
    def __init__(self, n):
        self.n = n
        self.i = 0

    def next(self):
        r = self.i % self.n
        self.i += 1
        return r


def build_program(layers, final_norm, n_tiles=NT):
    nc = bass.Bass("TRN2", target_bir_lowering=False)
    dt_in = lambda name, shape: nc.dram_tensor(name, shape, F32, kind="ExternalInput").ap()
    xT_d = dt_in("xT", [128, KC, NTOK])
    cT_d = dt_in("cT", [128, KC, BPC])
    wmod_d = dt_in("w_mod", [DEPTH, D, 6 * D])
    win_d = dt_in("w_in", [DEPTH, D, IN_COLS])
    wout_d = dt_in("w_out", [DEPTH, D, D])
    wup_d = dt_in("w_up", [DEPTH, D, 2 * DFF])
    wdn_d = dt_in("w_down", [DEPTH, DFF, D])
    cst_d = dt_in("cst", [128, NCST])
    vnb_d = dt_in("vnb", [128, DEPTH, 512])
    bsT_d = dt_in("bsT", [128, DEPTH, 4, 128])
    wsT_d = dt_in("wsT", [DEPTH, 128, 8, 128])
    yT_d = nc.dram_tensor("yT", [128, KC, NTOK], F32, kind="ExternalOutput").ap()

    A = nc.alloc_sbuf_tensor
    xT = A("xT_sb", [128, KC, T], F32)
    hT = A("hT_sb", [128, KC, T], BF16)
    arena = A("arena", [128, ARENA_BYTES // 4], F32)
    zT = A("zT_sb", [128, 4, T + 8], F32)
    wring = A("wring", [128, NSLOT * SLOT_ELEMS], BF16)
    NSQ = 6
    sqr = A("sqr", [128, NSQ, SUB], BF16)
    NTMP = 3
    tmpr = A("tmpr", [128, NTMP, SUB], F32)
    NRS = 4
    rstd = A("rstd", [128, NRS, SUB], F32)
    cst = A("cst_sb", [128, NCST], F32)
    vnb = A("vnb_sb", [128, DEPTH, 512], F32)
    bsT = A("bsT_sb", [128, DEPTH, 4, 128], F32)
    wsb = A("wsb_sb", [128, DEPTH, 8, 128], BF16)
    modt = A("modt", [128, DEPTH, 48, BPC], F32)
    amod = A("amod", [128, DEPTH, 2, KC, BPC], F32)
    cact = A("cact", [128, KC, BPC], F32)
    ones = A("ones", [128, 128], BF16)
    epst = A("epst", [128, 1], F32)
    junk1 = A("junk1", [128, 1], F32)
    nhalf = A("nhalf", [128, 1], F32)
    mbuf = A("mbuf", [128, 2, 1024], F32)
    zsave = A("zsave", [128, DEPTH, 4, 2], F32)
    ssv = A("ssv", [128, 8], F32)
    rvv = A("rvv", [128, 8], F32)

    psum = [nc.alloc_psum_tensor(f"ps{i}", [128, SUB], F32) for i in range(8)]

    sem_names = ["pe", "act", "dve", "pool", "w0", "w1", "w2", "dx0", "dx1", "dy0", "dy1", "dc", "dm0", "dm1", "dm2", "w3"]
    sems = {n: nc.alloc_semaphore(n) for n in sem_names}
    engs = {"pe": nc.tensor, "act": nc.scalar, "dve": nc.vector, "pool": nc.gpsimd, "sp": nc.sync}
    S = Sched(engs, sems)

    def akeys(off, nbytes):
        return [("A", i) for i in range(off // 1024, (off + nbytes + 1023) // 1024)]

    def af32(off, n=SUB):
        return arena[:, off // 4: off // 4 + n]

    def abf(off, n=SUB):
        return arena[:, off // 4: off // 4 + n // 2].bitcast(BF16)

    def hc_off(oc, s):
        return A_HC + (oc * T + s * SUB) * 4

    def gu_off(oc, s):
        return A_GU + (oc * T + s * SUB) * 4

    def vn_off(tt):
        return A_VN + tt * 1024

    def ymix_off(k, s):
        return A_YMIX + (k * T + s * SUB) * 2

    def act_off(j, s):
        return A_ACT + (j * T + s * SUB) * 2

    def sg_off(i):
        return A_SG + i * 2048

    def yout_off(c, s):
        return A_YOUT + (c * T + s * SUB) * 4

    def cs(col):
        return cst[:, col:col + 1]

    def ps_key(b):
        return ("ps", b)

    sq_ring = Ring(NSQ)
    tmp_ring = Ring(NTMP)
    rs_ring = Ring(NRS)
    sg_ring = Ring(NSG)
    main_pair = Ring(2)
    SINGLE_BANKS = [0, 1, 2, 3, 6]
    single_state = {"i": 0}

    def next_single_bank():
        banks = SINGLE_BANKS if mod_pending else SINGLE_BANKS + [7]
        single_state["i"] += 1
        return banks[single_state["i"] % len(banks)]
    stat_bank = Ring(2)
    misc_bank = Ring(1)

    fills = []
    for ti in range(n_tiles):
        for l in layers:
            for f in range(5):
                fills.append(("in", l, f))
            for f in range(2):
                fills.append(("out", l, f))
            for f in range(11):
                fills.append(("up", l, f))
            for f in range(4):
                fills.append(("down", l, f))
    wstate = {"issued": 0, "cur": 0, "oldest": 0}
    IN_ORDER = [4, 3, 2, 1, 0]

    def slot_view(slot, k, n):
        return wring[:, slot * SLOT_ELEMS: slot * SLOT_ELEMS + k * n].rearrange("p (k n) -> p k n", k=k)

    def issue_fill(i):
        kind, l, f = fills[i]
        slot = i % NSLOT
        if kind == "in":
            g = IN_ORDER[f]
            src = win_d[l].rearrange("(k p) n -> p k n", p=128)[:, :, g * 512:(g + 1) * 512]
            fns = [lambda e, src=src: e.dma_start(out=slot_view(slot, 8, 512), in_=src)]
        elif kind == "out":
            src = wout_d[l].rearrange("(k p) n -> p k n", p=128)[:, :, f * 512:(f + 1) * 512]
            fns = [lambda e, src=src: e.dma_start(out=slot_view(slot, 8, 512), in_=src)]
        elif kind == "up":
            wv = wup_d[l].rearrange("(k p) n -> p k n", p=128)
            sg_ = wv[:, :, f * 256:(f + 1) * 256]
            su_ = wv[:, :, DFF + f * 256: DFF + (f + 1) * 256]
            dst = slot_view(slot, 8, 512)
            fns = [lambda e, sg_=sg_, dst=dst: e.dma_start(out=dst[:, :, 0:256], in_=sg_),
                   lambda e, su_=su_, dst=dst: e.dma_start(out=dst[:, :, 256:512], in_=su_)]
        else:
            src = wdn_d[l].rearrange("(k p) n -> p k n", p=128)[:, :, f * 256:(f + 1) * 256]
            fns = [lambda e, src=src: e.dma_start(out=slot_view(slot, FC, 256), in_=src)]
        S.op("pool", fns, writes=[("w", slot)], sem=f"w{slot}", inc=16, inc_each=True)

    def acquire(kind, l, f, hold=False):
        i = wstate["cur"]
        assert fills[i] == (kind, l, f), (fills[i], kind, l, f)
        if not hold:
            wstate["oldest"] = i
        while wstate["issued"] < len(fills) and wstate["issued"] <= wstate["oldest"] + NSLOT - 1:
            issue_fill(wstate["issued"])
            wstate["issued"] += 1
        wstate["cur"] += 1
        return i % NSLOT

    deferred = []

    def flush_deferred():
        for d in deferred:
            d()
        deferred.clear()

    mod_pending = []
    mring = Ring(2)
    MODBANK = 7

    def mod_unit():
        if not mod_pending:
            return
        l, oc = mod_pending.pop(0)
        bi = mring.next()
        mv = mbuf[:, bi, :].rearrange("p (k n) -> p k n", k=8)
        src = wmod_d[l].rearrange("(k p) n -> p k n", p=128)[:, :, oc * 128:(oc + 1) * 128]
        S.op("sp", lambda e: e.dma_start(out=mv, in_=src), writes=[("mbuf", bi)], sem=f"dm{1 + bi}", inc=16)
        fns = [lambda e, k=k: e.matmul(psum[MODBANK][:, 0:BPC], mv[:, k, :], cact[:, k, :],
                                       start=(k == 0), stop=(k == KC - 1)) for k in range(KC)]
        S.op("pe", fns, reads=[("mbuf", bi), "cact"], writes=[ps_key(MODBANK)])
        S.op("dve", lambda e: e.tensor_scalar(out=modt[:, l, oc, :], in0=psum[MODBANK][:, 0:BPC],
                                              scalar1=cs(l * PL + O_BMOD + oc), scalar2=None, op0=ALU.add),
             reads=[ps_key(MODBANK), "cst"], writes=[("modc", l, oc)])
        if oc % 8 == 7 and oc // 8 in (1, 4):
            make_amod(l, 0 if oc // 8 == 1 else 1)

    def make_amod(l, which):
        m_sc, o_n = ((1, O_NMIX), (4, O_NFFN))[which]
        base = l * PL
        for b in range(BPC):
            S.op("dve", lambda e, b=b: e.scalar_tensor_tensor(
                out=amod[:, l, which, :, b], in0=modt[:, l, m_sc * 8:(m_sc + 1) * 8, b], scalar=1.0,
                in1=cst[:, base + o_n: base + o_n + 8], op0=ALU.add, op1=ALU.mult),
                reads=[("modc", l, m_sc * 8 + c) for c in range(KC)] + ["cst"],
                writes=[("amod", l, which, b)])

    def proj_group(slot, wsel, K, rhs_fn, rhs_keys):
        pr = main_pair.next()
        banks = (2 * pr, 2 * pr + 1)
        fns = []
        for k in range(K):
            for s in range(NS):
                fns.append(lambda e, k=k, s=s: e.matmul(psum[banks[s]][:], wsel(k), rhs_fn(k, s),
                                                        start=(k == 0), stop=(k == K - 1)))
        S.op("pe", fns, reads=[("w", slot)] + rhs_keys, writes=[ps_key(banks[0]), ps_key(banks[1])])
        mod_unit()
        return banks

    def proj_single(slot, wsel, K, rhs_fn, rhs_keys):
        bank = next_single_bank()
        fns = [lambda e, k=k: e.matmul(psum[bank][:], wsel(k), rhs_fn(k), start=(k == 0), stop=(k == K - 1))
               for k in range(K)]
        S.op("pe", fns, reads=[("w", slot)] + rhs_keys, writes=[ps_key(bank)])
        mod_unit()
        return bank

    def stat_matmul(bank, src_ap, src_keys, first, last):
        S.op("pe", lambda e: e.matmul(psum[bank][:], ones[:], src_ap, start=first, stop=last),
             reads=src_keys + ["ones"], writes=[ps_key(bank)])

    def square_to_ring(src_ap, src_keys):
        i = sq_ring.next()
        S.op("act", lambda e: e.activation(out=sqr[:, i, :], in_=src_ap, func=AF.Square),
             reads=src_keys, writes=[("sq", i)])
        return sqr[:, i, :], [("sq", i)]

    def make_rstd(bank, inv_n):
        i = rs_ring.next()
        S.op("act", lambda e: e.activation(out=rstd[:, i, :], in_=psum[bank][:], func=AF.Ln,
                                           bias=epst[:, 0:1], scale=inv_n),
             reads=[ps_key(bank), "eps"], writes=[("rs", i)])
        S.op("act", lambda e: e.activation(out=rstd[:, i, :], in_=rstd[:, i, :], func=AF.Exp, scale=-0.5),
             reads=[("rs", i)], writes=[("rs", i)])
        return i

    def preload_sqrt_table():
        S.op("act", lambda e: e.activation(out=junk1[:, 0:1], in_=epst[:, 0:1], func=AF.Ln),
             reads=["eps"], writes=["junk1"])
        S.op("act", lambda e: e.activation(out=junk1[:, 0:1], in_=junk1[:, 0:1], func=AF.Exp),
             reads=["junk1"], writes=["junk1"])

    def xs(c, s):
        return xT[:, c, s * SUB:(s + 1) * SUB]

    def x_stats_s(s):
        bank = 4 + stat_bank.next()
        for c in range(KC):
            sap, sk = square_to_ring(xs(c, s), [("x", c, s)])
            stat_matmul(bank, sap, sk, c == 0, c == KC - 1)
        return make_rstd(bank, 1.0 / D)

    def norm_half(l, b, which, s, r):
        sh_m = 0 if which == 0 else 3
        assert ("amod", l, which, b) in S.lastw and ("modc", l, sh_m * 8 + KC - 1) in S.lastw
        for c in range(KC):
            ti_ = tmp_ring.next()
            S.op("dve", lambda e, c=c, ti_=ti_: e.tensor_tensor(
                out=tmpr[:, ti_, :], in0=xs(c, s), in1=rstd[:, r, :], op=ALU.mult),
                reads=[("x", c, s), ("rs", r)], writes=[("tmp", ti_)])
            S.op("act", lambda e, c=c, ti_=ti_: e.activation(
                out=hT[:, c, s * SUB:(s + 1) * SUB], in_=tmpr[:, ti_, :], func=AF.Identity,
                bias=modt[:, l, sh_m * 8 + c, b:b + 1], scale=amod[:, l, which, c, b:b + 1]),
                reads=[("tmp", ti_), ("modc", l, sh_m * 8 + c), ("amod", l, which, b)], writes=[("h", c, s)])

    def x_update_s(l, b, m_g, oc, s, bank, sbank_x):
        assert ("modc", l, m_g * 8 + oc) in S.lastw
        S.op("dve", lambda e: e.scalar_tensor_tensor(
            out=xs(oc, s), in0=psum[bank][:], scalar=modt[:, l, m_g * 8 + oc, b:b + 1], in1=xs(oc, s),
            op0=ALU.mult, op1=ALU.add),
            reads=[ps_key(bank), ("x", oc, s), ("modc", l, m_g * 8 + oc)], writes=[("x", oc, s)])
        sap, sk = square_to_ring(xs(oc, s), [("x", oc, s)])
        deferred.append(lambda: stat_matmul(sbank_x[s], sap, sk, oc == 0, oc == KC - 1))

    def residual_proj(kind, l, b, m_g, nfills, cpf, K, rhs_fn, rhs_keys_fn, next_norm, all_souter):
        sbank_x = [4 + stat_bank.next() for _ in range(NS)]
        for f in range(nfills):
            slot = acquire(kind, l, f)
            wv = slot_view(slot, K, cpf * 128)
            last = f == nfills - 1
            if last or all_souter:
                for s in range(NS):
                    for ocl in range(cpf):
                        oc = f * cpf + ocl
                        bank = proj_single(slot, lambda k, ocl=ocl: wv[:, k, ocl * 128:(ocl + 1) * 128], K,
                                           lambda k, s=s: rhs_fn(k, s), rhs_keys_fn(s))
                        flush_deferred()
                        if last and s == 1 and ocl == 0:
                            next_norm(0, make_rstd(sbank_x[0], 1.0 / D))
                        x_update_s(l, b, m_g, oc, s, bank, sbank_x)
            else:
                for ocl in range(cpf):
                    oc = f * cpf + ocl
                    banks = proj_group(slot, lambda k, ocl=ocl: wv[:, k, ocl * 128:(ocl + 1) * 128], K,
                                       rhs_fn, rhs_keys_fn(0) + rhs_keys_fn(1))
                    flush_deferred()
                    for s in range(NS):
                        x_update_s(l, b, m_g, oc, s, banks[s], sbank_x)
        flush_deferred()
        r1 = make_rstd(sbank_x[1], 1.0 / D)
        late.append(lambda: next_norm(1, r1))

    late = []

    def run_late():
        while late:
            late.pop(0)()

    def h_rhs(k, s):
        return hT[:, k, s * SUB:(s + 1) * SUB]

    def h_keys():
        return [("h", k, s) for k in range(KC) for s in range(NS)]

    def load_x(ti, s):
        S.op("sp", [lambda e, c=c: e.dma_start(out=xs(c, s), in_=xT_d[:, c, ti * T + s * SUB: ti * T + (s + 1) * SUB])
                    for c in range(KC)],
             writes=[("x", c, s) for c in range(KC)], sem=f"dx{s}", inc=16, inc_each=True)

    S.op("sp", [lambda e: e.dma_start(out=cst[:], in_=cst_d),
                lambda e: e.dma_start(out=cact[:], in_=cT_d),
                lambda e: e.dma_start(out=vnb[:], in_=vnb_d),
                lambda e: e.dma_start(out=bsT[:], in_=bsT_d)],
         writes=["cst", "cact", "vnb", "bsT"], sem="dc", inc=16, inc_each=True)
    for s in range(NS):
        load_x(0, s)
    S.op("dve", lambda e: e.memset(epst[:], EPS), writes=["eps"])
    S.op("dve", lambda e: e.memset(nhalf[:], -0.5), writes=["nhalf"])
    S.op("dve", lambda e: e.memset(ones[:], 1.0), writes=["ones"])
    S.op("act", lambda e: e.activation(out=cact[:], in_=cact[:], func=AF.Silu), reads=["cact"], writes=["cact"])
    for l in layers:
        wsf = arena[:, A_WSF // 4: A_WSF // 4 + 1024].rearrange("p (h i) -> p h i", h=8)
        S.op("sp", lambda e, l=l: e.dma_start(out=wsf, in_=wsT_d[l]), writes=akeys(A_WSF, 4096), sem="dc", inc=16)
        S.op("pool", lambda e, l=l: e.affine_select(out=wsb[:, l, :, :], in_=wsf, pattern=[[0, 8], [1, 128]],
                                                    compare_op=ALU.is_ge, fill=0.0, base=0, channel_multiplier=-1),
             reads=akeys(A_WSF, 4096), writes=[("wsb", l)])
    rs0 = [x_stats_s(s) for s in range(NS)]
    dm_ring = Ring(3)
    l0 = layers[0]
    up_list = [(l0, slab) for slab in range(4)] if not DBG_MOD_UPFRONT else [(l, slab) for l in layers for slab in range(12)]
    for l0, slab in up_list:
        di = dm_ring.next()
        off = A_WMOD + di * 16384
        sv = arena[:, off // 4: off // 4 + 4096].rearrange("p (k n) -> p k n", k=8)
        src = wmod_d[l0].rearrange("(k p) n -> p k n", p=128)[:, :, slab * 512:(slab + 1) * 512]
        S.op("sp", lambda e, sv=sv, src=src: e.dma_start(out=sv, in_=src),
             writes=akeys(off, 16384), sem=f"dm{di}", inc=16)
        fns = []
        for j in range(4):
            for k in range(KC):
                fns.append(lambda e, sv=sv, j=j, k=k: e.matmul(
                    psum[MODBANK][:, j * BPC:(j + 1) * BPC], sv[:, k, j * 128:(j + 1) * 128], cact[:, k, :],
                    start=(k == 0), stop=(k == KC - 1)))
        S.op("pe", fns, reads=akeys(off, 16384) + ["cact"], writes=[ps_key(MODBANK)])
        cb = l0 * PL + O_BMOD + slab * 4
        S.op("dve", lambda e, slab=slab, cb=cb: e.tensor_tensor(
            out=modt[:, l0, slab * 4:(slab + 1) * 4, :],
            in0=psum[MODBANK][:, 0:4 * BPC].rearrange("p (o b) -> p o b", b=BPC),
            in1=cst[:, cb: cb + 4].unsqueeze(2).to_broadcast([128, 4, BPC]), op=ALU.add),
            reads=[ps_key(MODBANK), "cst"], writes=[("modc", l0, slab * 4 + j) for j in range(4)])
    l0 = layers[0]
    make_amod(l0, 0)
    if DBG_MOD_UPFRONT:
        make_amod(l0, 1)
        for l in layers[1:]:
            make_amod(l, 0)
            make_amod(l, 1)
    else:
        for l in layers:
            for oc in range(48):
                if l == l0 and oc < 16:
                    continue
                mod_pending.append((l, oc))

    def mixer(l, ti, next_norm):
        b = ti // TPB
        half = ti % TPB
        base = l * PL
        hk = h_keys()

        def v_group(slot, wv, tt):
            bank = next_single_bank()
            s_ = tt // 4
            fns = [lambda e, k=k: e.matmul(psum[bank][:], hT[:, k, tt * 128:(tt + 1) * 128], wv[:, k, :],
                                           start=(k == 0), stop=(k == KC - 1)) for k in range(KC)]
            S.op("pe", fns, reads=[("w", slot)] + [("h", k, s_) for k in range(KC)], writes=[ps_key(bank)])
            mod_unit()
            ti_ = tmp_ring.next()
            S.op("act", lambda e: e.activation(out=tmpr[:, ti_, :], in_=psum[bank][:], func=AF.Gelu_apprx_tanh),
                 reads=[ps_key(bank)], writes=[("tmp", ti_)])
            qi = sq_ring.next()
            S.op("act", lambda e: e.activation(out=sqr[:, qi, :], in_=tmpr[:, ti_, :],
                                               func=AF.Square, accum_out=ssv[:, tt:tt + 1]),
                 reads=[("tmp", ti_)], writes=[("sq", qi), ("ss", tt)])
            S.op("dve", lambda e: e.tensor_scalar(out=rvv[:, tt:tt + 1], in0=ssv[:, tt:tt + 1],
                                                  scalar1=1.0 / 512, scalar2=EPS, op0=ALU.mult, op1=ALU.add),
                 reads=[("ss", tt)], writes=[("rv", tt)])
            S.op("pool", lambda e: e.tensor_tensor(out=rvv[:, tt:tt + 1], in0=rvv[:, tt:tt + 1],
                                                   in1=nhalf[:, 0:1], op=ALU.pow),
                 reads=[("rv", tt), "nhalf"], writes=[("rv", tt)])
            S.op("dve", lambda e: e.scalar_tensor_tensor(
                out=abf(vn_off(tt)), in0=tmpr[:, ti_, :], scalar=rvv[:, tt:tt + 1], in1=vnb[:, l, :],
                op0=ALU.mult, op1=ALU.mult),
                reads=[("tmp", ti_), ("rv", tt), "vnb"], writes=akeys(vn_off(tt), 1024))

        slot_v = acquire("in", l, 0)
        wv_v = slot_view(slot_v, 8, 512)
        slot_u = acquire("in", l, 1, hold=True)
        wv_u = slot_view(slot_u, 8, 512)
        for s in range(NS):
            for ttl in range(4):
                v_group(slot_v, wv_v, s * 4 + ttl)
                if s == 0 and ttl == 1:
                    run_late()
            hks = [("h", k, s) for k in range(KC)]
            for oc in range(4):
                bank = proj_single(slot_u, lambda k, oc=oc: wv_u[:, k, oc * 128:(oc + 1) * 128], KC,
                                   lambda k, s=s: h_rhs(k, s), hks)
                S.op("act", lambda e, oc=oc, s=s, bank=bank: e.activation(
                    out=af32(gu_off(oc, s)), in_=psum[bank][:], func=AF.Gelu_apprx_tanh),
                    reads=[ps_key(bank)], writes=akeys(gu_off(oc, s), 2048))

        preload_sqrt_table()
        sbank_b = [4 + stat_bank.next() for _ in range(NS)]
        sgu_list = [(s, fc) for s in range(NS) for fc in range(4)]

        def sgu_group():
            if not sgu_list:
                return
            s, fc = sgu_list.pop(0)
            bank = 6 + misc_bank.next()
            fns = []
            rk = [("wsb", l)]
            for ttl in range(4):
                tt = s * 4 + ttl
                rk += akeys(vn_off(tt), 1024)
                for hh in range(2):
                    h = 2 * fc + hh
                    fns.append(lambda e, ttl=ttl, tt=tt, hh=hh, h=h: e.matmul(
                        psum[bank][hh * 64:(hh + 1) * 64, ttl * 128:(ttl + 1) * 128],
                        abf(vn_off(tt))[:, h * 64:(h + 1) * 64], wsb[:, l, h, :], start=True, stop=True))
            S.op("pe", fns, reads=rk, writes=[ps_key(bank)])
            flush_deferred()
            ti_ = tmp_ring.next()
            S.op("dve", lambda e: e.tensor_tensor(
                out=tmpr[:, ti_, :].rearrange("p (a i) -> p a i", a=4),
                in0=psum[bank][:].rearrange("p (a i) -> p a i", a=4),
                in1=bsT[:, l, fc, :].unsqueeze(1).to_broadcast([128, 4, 128]), op=ALU.add),
                reads=[ps_key(bank), "bsT"], writes=[("tmp", ti_)])
            gk = akeys(gu_off(fc, s), 2048)
            S.op("pool", lambda e: e.tensor_tensor(
                out=af32(gu_off(fc, s)), in0=tmpr[:, ti_, :], in1=af32(gu_off(fc, s)), op=ALU.mult),
                reads=[("tmp", ti_)] + gk, writes=gk)
            sap, sk = square_to_ring(af32(gu_off(fc, s)), gk)
            deferred.append(lambda: stat_matmul(sbank_b[s], sap, sk, fc == 0, fc == 3))

        slot = acquire("in", l, 2)
        wv = slot_view(slot, 8, 512)
        for oc in range(4):
            banks = proj_group(slot, lambda k, oc=oc: wv[:, k, oc * 128:(oc + 1) * 128], KC, h_rhs, hk)
            for s in range(NS):
                S.op("act", lambda e, oc=oc, s=s: e.activation(out=af32(hc_off(oc, s)), in_=psum[banks[s]][:],
                                                               func=AF.Identity),
                     reads=[ps_key(banks[s])], writes=akeys(hc_off(oc, s), 2048))
            sgu_group()
        if half == 0:
            S.op("pool", lambda e: e.memset(zT[:, :, 0:2], 0.0), writes=[("zpad",)])
        else:
            S.op("pool", lambda e: e.tensor_copy(out=zT[:, :, 0:2], in_=zsave[:, l, :, :]),
                 reads=[("zsave", l)], writes=[("zpad",)])
        slot = acquire("in", l, 3)
        wv = slot_view(slot, 8, 512)
        for oc in range(4):
            banks = proj_group(slot, lambda k, oc=oc: wv[:, k, oc * 128:(oc + 1) * 128], KC, h_rhs, hk)
            flush_deferred()
            for s in range(NS):
                S.op("dve", lambda e, oc=oc, s=s: e.tensor_tensor(
                    out=zT[:, oc, 2 + s * SUB: 2 + (s + 1) * SUB], in0=psum[banks[s]][:], in1=af32(hc_off(oc, s)),
                    op=ALU.mult),
                    reads=[ps_key(banks[s])] + akeys(hc_off(oc, s), 2048), writes=[("z", oc, s)])
            for s in range(NS):
                zk = [("z", oc, s), ("z", oc, s - 1) if s > 0 else ("zpad",)]
                ck = akeys(hc_off(oc, s), 2048)
                S.op("act", lambda e, oc=oc, s=s: e.activation(
                    out=af32(hc_off(oc, s)), in_=zT[:, oc, s * SUB: (s + 1) * SUB], func=AF.Identity,
                    scale=cs(base + O_CONV + 0 * 4 + oc)),
                    reads=zk + ["cst"], writes=ck)
                for tap in (1, 2):
                    S.op("dve", lambda e, oc=oc, s=s, tap=tap: e.scalar_tensor_tensor(
                        out=af32(hc_off(oc, s)), in0=zT[:, oc, tap + s * SUB: tap + (s + 1) * SUB],
                        scalar=cs(base + O_CONV + tap * 4 + oc), in1=af32(hc_off(oc, s)),
                        op0=ALU.mult, op1=ALU.add),
                        reads=zk + ck + ["cst"], writes=ck)
            sgu_group()
        if half < TPB - 1:
            S.op("pool", lambda e: e.tensor_copy(out=zsave[:, l, :, :], in_=zT[:, :, T:T + 2]),
                 reads=[("z", oc, NS - 1) for oc in range(4)], writes=[("zsave", l)])
        assert not sgu_list
        slot = acquire("in", l, 4)
        wv = slot_view(slot, 8, 512)
        sbank_a = None
        for oc in range(4):
            banks = proj_group(slot, lambda k, oc=oc: wv[:, k, oc * 128:(oc + 1) * 128], KC, h_rhs, hk)
            flush_deferred()
            if oc == 0:
                for s2 in range(NS):
                    r_b = make_rstd(sbank_b[s2], 1.0 / 512)
                    for fc in range(4):
                        S.op("dve", lambda e, fc=fc, s2=s2, r_b=r_b: e.scalar_tensor_tensor(
                            out=abf(ymix_off(4 + fc, s2)), in0=af32(gu_off(fc, s2)), scalar=cs(base + O_ONB + fc),
                            in1=rstd[:, r_b, :], op0=ALU.mult, op1=ALU.mult),
                            reads=akeys(gu_off(fc, s2), 2048) + [("rs", r_b), "cst"],
                            writes=akeys(ymix_off(4 + fc, s2), 1024))
                sbank_a = [4 + stat_bank.next() for _ in range(NS)]
            for s in range(NS):
                ck = akeys(hc_off(oc, s), 2048)
                S.op("dve", lambda e, oc=oc, s=s: e.tensor_tensor(
                    out=af32(hc_off(oc, s)), in0=psum[banks[s]][:], in1=af32(hc_off(oc, s)), op=ALU.mult),
                    reads=[ps_key(banks[s])] + ck, writes=ck)
                sap, sk = square_to_ring(af32(hc_off(oc, s)), ck)
                deferred.append(lambda s=s, sap=sap, sk=sk, oc=oc: stat_matmul(sbank_a[s], sap, sk, oc == 0, oc == 3))
        def ya_norm():
            flush_deferred()
            for s2 in range(NS):
                r_a = make_rstd(sbank_a[s2], 1.0 / 512)
                for oc in range(4):
                    S.op("dve", lambda e, oc=oc, s2=s2, r_a=r_a: e.scalar_tensor_tensor(
                        out=abf(ymix_off(oc, s2)), in0=af32(hc_off(oc, s2)), scalar=cs(base + O_ONA + oc),
                        in1=rstd[:, r_a, :], op0=ALU.mult, op1=ALU.mult),
                        reads=akeys(hc_off(oc, s2), 2048) + [("rs", r_a), "cst"],
                        writes=akeys(ymix_off(oc, s2), 1024))
        ya_norm()
        residual_proj("out", l, b, 2, 2, 4, KC, lambda k, s: abf(ymix_off(k, s)),
                      lambda s: [k_ for k in range(KC) for k_ in akeys(ymix_off(k, s), 1024)], next_norm, True)

    def ffn(l, ti, next_norm):
        b = ti // TPB
        hk = h_keys()
        slots = [acquire("up", l, 0), acquire("up", l, 1, hold=True)]
        for s in range(NS):
            hks = [("h", k, s) for k in range(KC)]
            for f in range(2):
                wv = slot_view(slots[f], 8, 512)
                for jl in range(2):
                    j = 2 * f + jl
                    bank = proj_single(slots[f], lambda k, jl=jl, wv=wv: wv[:, k, jl * 128:(jl + 1) * 128], KC,
                                       lambda k, s=s: h_rhs(k, s), hks)
                    gi = sg_ring.next()
                    S.op("act", lambda e, gi=gi, bank=bank: e.activation(
                        out=af32(sg_off(gi)), in_=psum[bank][:], func=AF.Silu),
                        reads=[ps_key(bank)], writes=akeys(sg_off(gi), 2048))
                    bank = proj_single(slots[f], lambda k, jl=jl, wv=wv: wv[:, k, 256 + jl * 128: 256 + (jl + 1) * 128],
                                       KC, lambda k, s=s: h_rhs(k, s), hks)
                    S.op("dve", lambda e, s=s, j=j, gi=gi, bank=bank: e.tensor_tensor(
                        out=abf(act_off(j, s)), in0=psum[bank][:], in1=af32(sg_off(gi)), op=ALU.mult),
                        reads=[ps_key(bank)] + akeys(sg_off(gi), 2048), writes=akeys(act_off(j, s), 1024))
                    if s == 0 and f == 0 and jl == 0:
                        run_late()
        for f in range(2, 11):
            slot = acquire("up", l, f)
            wv = slot_view(slot, 8, 512)
            for jl in range(2):
                j = 2 * f + jl
                banks = proj_group(slot, lambda k, jl=jl: wv[:, k, jl * 128:(jl + 1) * 128], KC, h_rhs, hk)
                sgi = []
                for s in range(NS):
                    gi = sg_ring.next()
                    sgi.append(gi)
                    S.op("act", lambda e, s=s, gi=gi: e.activation(out=af32(sg_off(gi)), in_=psum[banks[s]][:],
                                                                   func=AF.Silu),
                         reads=[ps_key(banks[s])], writes=akeys(sg_off(gi), 2048))
                banks = proj_group(slot, lambda k, jl=jl: wv[:, k, 256 + jl * 128: 256 + (jl + 1) * 128], KC, h_rhs, hk)
                for s in range(NS):
                    S.op("dve", lambda e, s=s, j=j, gi=sgi[s]: e.tensor_tensor(
                        out=abf(act_off(j, s)), in0=psum[banks[s]][:], in1=af32(sg_off(gi)), op=ALU.mult),
                        reads=[ps_key(banks[s])] + akeys(sg_off(sgi[s]), 2048), writes=akeys(act_off(j, s), 1024))
        preload_sqrt_table()
        residual_proj("down", l, b, 5, 4, 2, FC, lambda k, s: abf(act_off(k, s)),
                      lambda s: [k_ for j in range(FC) for k_ in akeys(act_off(j, s), 1024)], next_norm, False)

    def final_half(ti, s, r):
        if final_norm:
            hT32 = hT[:, :, :].rearrange("p k t -> p (k t)").bitcast(F32)
            def yo(c):
                return hT32[:, c * SUB:(c + 1) * SUB] if s == 0 else af32(yout_off(c, 1))
            def yk(c):
                return [("h", c, 0), ("h", c, 1)] if s == 0 else akeys(yout_off(c, 1), 2048)
            for c in range(KC):
                S.op("dve", lambda e, c=c: e.scalar_tensor_tensor(
                    out=yo(c), in0=xs(c, s), scalar=cs(O_NFIN + c),
                    in1=rstd[:, r, :], op0=ALU.mult, op1=ALU.mult),
                    reads=[("x", c, s), ("rs", r), "cst"], writes=yk(c))
            S.op("sp", [lambda e, c=c: e.dma_start(out=yT_d[:, c, ti * T + s * SUB: ti * T + (s + 1) * SUB], in_=yo(c))
                        for c in range(KC)],
                 reads=[k_ for c in range(KC) for k_ in yk(c)],
                 sem=f"dy{s}", inc=16, inc_each=True)
        else:
            S.op("sp", [lambda e, c=c: e.dma_start(out=yT_d[:, c, ti * T + s * SUB: ti * T + (s + 1) * SUB],
                                                   in_=xs(c, s))
                        for c in range(KC)],
                 reads=[("x", c, s) for c in range(KC)], sem=f"dy{s}", inc=16, inc_each=True)
        if ti + 1 < n_tiles:
            load_x(ti + 1, s)

    for ti in range(n_tiles):
        b = ti // TPB
        for s in range(NS):
            r = rs0[s] if ti == 0 else x_stats_s(s)
            norm_half(layers[0], b, 0, s, r)
        for li, l in enumerate(layers):
            mixer(l, ti, lambda s, r, l=l: norm_half(l, b, 1, s, r))
            if li + 1 < len(layers):
                nn = lambda s, r, l2=layers[li + 1]: norm_half(l2, b, 0, s, r)
            else:
                nn = lambda s, r: final_half(ti, s, r)
            ffn(l, ti, nn)
        run_late()
    S.wait_all("sp", ["dy0", "dy1"])
    return nc


def _prep_inputs(x, c, w_mod, b_mod, norm_mix, w_in, conv_w, v_norm, w_s, b_s, out_norm_a, out_norm_b,
                 w_out, norm_ffn, w_up, w_down, norm_final):
    f = lambda a: np.ascontiguousarray(np.asarray(a, dtype=np.float32))
    x, c = f(x), f(c)
    pp = lambda v: np.asarray(v, np.float32).reshape(-1, 128).T
    cst = np.zeros((128, NCST), np.float32)
    for l in range(DEPTH):
        base = l * PL
        cst[:, base + O_NMIX: base + O_NMIX + 8] = pp(norm_mix[l])
        cst[:, base + O_NFFN: base + O_NFFN + 8] = pp(norm_ffn[l])
        for k in range(3):
            cst[:, base + O_CONV + 4 * k: base + O_CONV + 4 * k + 4] = pp(conv_w[l][k])
        cst[:, base + O_ONA: base + O_ONA + 4] = pp(out_norm_a[l])
        cst[:, base + O_ONB: base + O_ONB + 4] = pp(out_norm_b[l])
        cst[:, base + O_BMOD: base + O_BMOD + 48] = pp(b_mod[l])
    cst[:, O_NFIN: O_NFIN + 8] = pp(norm_final)
    vnb = np.ascontiguousarray(np.broadcast_to(np.asarray(v_norm, np.float32)[None, :, :], (128, DEPTH, 512)))
    bs = np.asarray(b_s, np.float32)
    bsT = np.repeat(bs.reshape(DEPTH, 4, 2, 1, 128), 64, axis=3).reshape(DEPTH, 4, 128, 128)
    bsT = np.ascontiguousarray(bsT.transpose(2, 0, 1, 3))
    wsT = np.ascontiguousarray(np.asarray(w_s, np.float32).transpose(0, 3, 1, 2))
    shared = {"w_mod": f(w_mod), "w_in": f(w_in), "w_out": f(w_out), "w_up": f(w_up), "w_down": f(w_down),
              "cst": cst, "vnb": vnb, "bsT": bsT, "wsT": wsT}
    in_maps = []
    for i in range(NCORE):
        xb = x[i * BPC:(i + 1) * BPC].reshape(NTOK, KC, 128)
        xT = np.ascontiguousarray(xb.transpose(2, 1, 0))
        cT = np.ascontiguousarray(c[i * BPC:(i + 1) * BPC].reshape(BPC, KC, 128).transpose(2, 1, 0))
        m = dict(shared)
        m["xT"] = xT
        m["cT"] = cT
        in_maps.append(m)
    return in_maps


def _gather(results):
    out = np.empty((BATCH, SEQ, D), np.float32)
    for i, r in enumerate(results):
        yT = np.asarray(r["yT"])
        out[i * BPC:(i + 1) * BPC] = yT.transpose(2, 1, 0).reshape(BPC, SEQ, D)
    return out


FUSED = True
DBG_MOD_UPFRONT = False
_cache = {}


def _prog(layers, final_norm):
    key = (tuple(layers), final_norm)
    if key not in _cache:
        _cache[key] = build_program(list(layers), final_norm)
    return _cache[key]


def kernel(**inputs):
    in_maps = _prep_inputs(**inputs)
    cores = list(range(NCORE))
    if FUSED:
        res = run_bass_kernel_spmd(_prog(range(DEPTH), True), in_maps, core_ids=cores)
        return _gather(res.results)
    for l in range(DEPTH):
        last = l == DEPTH - 1
        res = run_bass_kernel_spmd(_prog([l], last), in_maps, core_ids=cores)
        if not last:
            for m, r in zip(in_maps, res.results):
                m["xT"] = np.ascontiguousarray(r["yT"])
    return _gather(res.results)
```
